# Optimizing a Trainium2 kernel written in Bass

```python
import math
import jax, jax.numpy as jnp
from jax import lax
import numpy as np

D_MODEL = 2048
BATCH = 4
SEQ = 4096
DEPTH = 1

GRID_W = 64
CTX_LEN = 256

D_SSD = D_MODEL
SSD_HEAD_DIM = 64
SSD_HEADS = D_SSD // SSD_HEAD_DIM
SSD_GROUPS = 8
SSD_REP = SSD_HEADS // SSD_GROUPS
SSD_STATE = 128
SSD_CONV = 3
SSD_CHUNK = 128

D_POOL = D_MODEL
POOL_WINDOWS = (2, 4, 8, 16)
POOL_GROUPS = len(POOL_WINDOWS)
POOL_GW = D_POOL // POOL_GROUPS

D_FF = 5632
FFN_CONV = 3

EPS = 1e-6

COL_DT = 2 * SSD_HEADS
COL_BC = SSD_GROUPS * SSD_STATE
N_XBC = D_SSD + 2 * COL_BC
N_STATE_COLS = COL_DT + D_SSD + COL_BC
IN_SIZES = (COL_DT, N_XBC, D_SSD, D_POOL, D_MODEL, D_MODEL)
IN_COLS = sum(IN_SIZES)

kernel_name = "hybrid_ssd_pool_convglu_prefix_block"


def rms_norm(x, w):
    x32 = x.astype(jnp.float32)
    y = x32 * lax.rsqrt(jnp.mean(x32 * x32, axis=-1, keepdims=True) + EPS)
    return (y * w.astype(jnp.float32)).astype(x.dtype)


def modulate(x, w, shift, scale):
    return rms_norm(x, w) * (1 + scale) + shift


def dwconv_axis1(x, w, b):
    k = w.shape[0]
    n = x.shape[1]
    p = k // 2
    xp = jnp.pad(x, [(0, 0), (p, p)] + [(0, 0)] * (x.ndim - 2))
    out = xp[:, 0:n] * w[0] + b
    for j in range(1, k):
        out = out + xp[:, j:j + n] * w[j]
    return out


def centred_pool_residual(v, window):
    n = v.shape[-2]
    v32 = v.astype(jnp.float32)
    csum = jnp.cumsum(v32, axis=-2)
    prefix = jnp.pad(csum, [(0, 0)] * (v.ndim - 2) + [(1, 0), (0, 0)])
    t = jnp.arange(n)
    start = jnp.clip(t - window // 2, 0, n)
    end = jnp.clip(t + window - window // 2, 0, n)
    total = jnp.take(prefix, end, axis=-2) - jnp.take(prefix, start, axis=-2)
    count = (end - start).astype(jnp.float32)[:, None]
    return (total / count - v32).astype(v.dtype)


def pool_mixer(v, pool_w, pool_scale, grid):
    b, l, _ = v.shape
    if grid:
        v = v.reshape(b, l // GRID_W, GRID_W, D_POOL)
    groups = jnp.split(v, POOL_GROUPS, axis=-1)
    p = jnp.stack([centred_pool_residual(g, w) for g, w in zip(groups, POOL_WINDOWS)], axis=-2)
    p = jnp.einsum('...gi,gio->...go', p, pool_w).reshape(b, l, D_POOL)
    return p * pool_scale


def _ssd_chunks(xs, dt, A, B):
    b, l = xs.shape[:2]
    nc = l // SSD_CHUNK
    xd = (xs.astype(jnp.float32) * dt[..., None]).reshape(b, nc, SSD_CHUNK, SSD_GROUPS, SSD_REP, SSD_HEAD_DIM)
    a_cs = jnp.cumsum((dt * A).reshape(b, nc, SSD_CHUNK, SSD_GROUPS, SSD_REP), axis=2)
    bc = B.astype(jnp.float32).reshape(b, nc, SSD_CHUNK, SSD_GROUPS, SSD_STATE)
    return xd, a_cs, bc


def _ssd_carry(xd, a_cs, bc, h0):
    decay_to_end = jnp.exp(a_cs[:, :, -1:] - a_cs)
    states = jnp.einsum('bcsgn,bcsgr,bcsgrp->bcgrpn', bc, decay_to_end, xd)
    chunk_decay = jnp.exp(a_cs[:, :, -1])

    def step(h, inp):
        s, d = inp
        return h * d[..., None, None] + s, h

    h_final, h_in = lax.scan(step, h0, (jnp.moveaxis(states, 1, 0), jnp.moveaxis(chunk_decay, 1, 0)))
    return jnp.moveaxis(h_in, 0, 1), h_final


def ssd_scan(xs, dt, A, B, C, d_skip, h0):
    xd, a_cs, bc = _ssd_chunks(xs, dt, A, B)
    h_in, h_final = _ssd_carry(xd, a_cs, bc, h0)
    b, nc = xd.shape[:2]
    cc = C.astype(jnp.float32).reshape(b, nc, SSD_CHUNK, SSD_GROUPS, SSD_STATE)
    seg = a_cs[:, :, :, None] - a_cs[:, :, None]
    order = jnp.tril(jnp.ones((SSD_CHUNK, SSD_CHUNK), dtype=bool))[:, :, None, None]
    decay = jnp.exp(jnp.where(order, seg, -jnp.inf))
    cb = jnp.einsum('bclgn,bcsgn->bclsg', cc, bc)
    y_diag = jnp.einsum('bclsg,bclsgr,bcsgrp->bclgrp', cb, decay, xd)
    y_off = jnp.einsum('bclgn,bcgrpn,bclgr->bclgrp', cc, h_in, jnp.exp(a_cs))
    y = (y_diag + y_off).reshape(xs.shape) + d_skip[..., None] * xs.astype(jnp.float32)
    return y, h_final


def _flip(t):
    return jnp.flip(t, axis=1)


def _ssd_prepare(dt_raw, xbc, conv_w, conv_b, dt_bias, a_log):
    b, l, _ = xbc.shape
    xbc = jax.nn.silu(dwconv_axis1(xbc, conv_w, conv_b))
    xs = xbc[..., :D_SSD].reshape(b, l, SSD_GROUPS, SSD_REP, SSD_HEAD_DIM)
    B = xbc[..., D_SSD:D_SSD + COL_BC].reshape(b, l, SSD_GROUPS, SSD_STATE)
    rest = xbc[..., D_SSD + COL_BC:]
    dt = jax.nn.softplus(dt_raw.astype(jnp.float32).reshape(b, l, 2, SSD_GROUPS, SSD_REP)
                         + dt_bias.astype(jnp.float32).reshape(2, SSD_GROUPS, SSD_REP))
    A = -jnp.exp(a_log.astype(jnp.float32)).reshape(2, SSD_GROUPS, SSD_REP)
    return xs, B, rest, dt, A


def _zero_state(b):
    return jnp.zeros((b, SSD_GROUPS, SSD_REP, SSD_HEAD_DIM, SSD_STATE), jnp.float32)


def context_ssd_states(h, lp):
    b = h.shape[0]
    proj = h @ lp['w_in'][:, :N_STATE_COLS]
    xs, B, _, dt, A = _ssd_prepare(proj[..., :COL_DT], proj[..., COL_DT:],
                                   lp['ssd_conv_w'][:, :D_SSD + COL_BC], lp['ssd_conv_b'][:D_SSD + COL_BC],
                                   lp['dt_bias'], lp['a_log'])
    xd, a_cs, bc = _ssd_chunks(xs, dt[:, :, 0], A[0], B)
    h_f = _ssd_carry(xd, a_cs, bc, _zero_state(b))[1]
    xd, a_cs, bc = _ssd_chunks(_flip(xs), _flip(dt[:, :, 1]), A[1], _flip(B))
    h_b = _ssd_carry(xd, a_cs, bc, _zero_state(b))[1]
    return h_f, h_b


def token_mixer(h, lp, h0_f, h0_b, grid):
    b, l, _ = h.shape
    proj = h @ lp['w_in']
    idx = np.cumsum(IN_SIZES)[:-1].tolist()
    dt_raw, xbc, z, v, g_ssd, g_pool = jnp.split(proj, idx, axis=-1)

    xs, B, c_flat, dt, A = _ssd_prepare(dt_raw, xbc, lp['ssd_conv_w'], lp['ssd_conv_b'], lp['dt_bias'], lp['a_log'])
    C = c_flat.reshape(b, l, SSD_GROUPS, SSD_STATE)
    d_skip = lp['d_skip'].astype(jnp.float32).reshape(2, SSD_GROUPS, SSD_REP)
    y_f, h_f = ssd_scan(xs, dt[:, :, 0], A[0], B, C, d_skip[0], h0_f)
    y_b, h_b = ssd_scan(_flip(xs), _flip(dt[:, :, 1]), A[1], _flip(B), _flip(C), d_skip[1], h0_b)
    y = (y_f + _flip(y_b)).reshape(b, l, D_SSD)
    gated = (y * jax.nn.silu(z.astype(jnp.float32))).reshape(b, l, SSD_GROUPS, D_SSD // SSD_GROUPS)
    y_ssd = rms_norm(gated, lp['ssd_norm_w'].reshape(SSD_GROUPS, -1)).reshape(b, l, D_SSD).astype(h.dtype)

    y_pool = pool_mixer(v, lp['pool_w'], lp['pool_scale'], grid)

    merged = (jax.nn.sigmoid(g_ssd) * (y_ssd @ lp['w_ssd_out'])
              + jax.nn.sigmoid(g_pool) * (y_pool @ lp['w_pool_out']))
    return merged @ lp['w_o'], h_f, h_b


def conv_ffn(h, lp, grid):
    b, l, _ = h.shape
    a, g = jnp.split(h @ lp['w_up'], 2, axis=-1)
    if grid:
        g = g.reshape(b, l // GRID_W, GRID_W, D_FF)
    g = dwconv_axis1(g, lp['ffn_conv_w'], lp['ffn_conv_b']).reshape(b, l, D_FF)
    return (jax.nn.gelu(g, approximate=False) * a) @ lp['w_down']


def setup_inputs(seed: int = 0) -> dict:
    key = jax.random.key(seed)
    ks = jax.random.split(key, 28)
    f32 = jnp.float32

    def nrm(k, shape, scale):
        return jax.random.normal(k, shape, f32) * scale

    dt0 = jnp.exp(jax.random.uniform(ks[10], (DEPTH, 2, SSD_HEADS), f32,
                                     minval=math.log(1e-3), maxval=math.log(1e-1)))
    return {
        "x": nrm(ks[0], (BATCH, SEQ, D_MODEL), 1.0),
        "c": nrm(ks[1], (BATCH, D_MODEL), 1.0),
        "ctx": nrm(ks[2], (BATCH, CTX_LEN, D_MODEL), 1.0),
        "c_ctx": nrm(ks[3], (D_MODEL,), 1.0),
        "w_ada": nrm(ks[4], (DEPTH, D_MODEL, 6 * D_MODEL), 0.5 * D_MODEL ** -0.5),
        "b_ada": nrm(ks[5], (DEPTH, 6 * D_MODEL), 0.01),
        "norm1_w": 1.0 + nrm(ks[6], (DEPTH, D_MODEL), 0.02),
        "w_in": nrm(ks[7], (DEPTH, D_MODEL, IN_COLS), D_MODEL ** -0.5),
        "ssd_conv_w": nrm(ks[8], (DEPTH, SSD_CONV, N_XBC), SSD_CONV ** -0.5),
        "ssd_conv_b": nrm(ks[9], (DEPTH, N_XBC), 0.01),
        "dt_bias": dt0 + jnp.log(-jnp.expm1(-dt0)),
        "a_log": jnp.log(jax.random.uniform(ks[11], (DEPTH, 2, SSD_HEADS), f32, minval=1.0, maxval=16.0)),
        "d_skip": 1.0 + nrm(ks[12], (DEPTH, 2, SSD_HEADS), 0.1),
        "ssd_norm_w": 1.0 + nrm(ks[13], (DEPTH, D_SSD), 0.02),
        "w_ssd_out": nrm(ks[14], (DEPTH, D_SSD, D_MODEL), D_SSD ** -0.5),
        "pool_w": nrm(ks[15], (DEPTH, POOL_GROUPS, POOL_GW, POOL_GW), POOL_GW ** -0.5),
        "pool_scale": 1.0 + nrm(ks[16], (DEPTH, D_POOL), 0.02),
        "w_pool_out": nrm(ks[17], (DEPTH, D_POOL, D_MODEL), D_POOL ** -0.5),
        "w_o": nrm(ks[18], (DEPTH, D_MODEL, D_MODEL), D_MODEL ** -0.5),
        "norm2_w": 1.0 + nrm(ks[19], (DEPTH, D_MODEL), 0.02),
        "w_up": nrm(ks[20], (DEPTH, D_MODEL, 2 * D_FF), D_MODEL ** -0.5),
        "ffn_conv_w": nrm(ks[21], (DEPTH, FFN_CONV, D_FF), FFN_CONV ** -0.5),
        "ffn_conv_b": nrm(ks[22], (DEPTH, D_FF), 0.01),
        "w_down": nrm(ks[23], (DEPTH, D_FF, D_MODEL), D_FF ** -0.5),
        "final_norm_w": 1.0 + nrm(ks[24], (D_MODEL,), 0.02),
    }


def reference(x, c, ctx, c_ctx, w_ada, b_ada, norm1_w, w_in, ssd_conv_w, ssd_conv_b, dt_bias, a_log,
              d_skip, ssd_norm_w, w_ssd_out, pool_w, pool_scale, w_pool_out, w_o, norm2_w, w_up,
              ffn_conv_w, ffn_conv_b, w_down, final_norm_w):
    x_lat = x
    x_ctx = ctx
    for i in range(DEPTH):
        lp = {
            'w_in': w_in[i], 'ssd_conv_w': ssd_conv_w[i], 'ssd_conv_b': ssd_conv_b[i],
            'dt_bias': dt_bias[i], 'a_log': a_log[i], 'd_skip': d_skip[i], 'ssd_norm_w': ssd_norm_w[i],
            'w_ssd_out': w_ssd_out[i], 'pool_w': pool_w[i], 'pool_scale': pool_scale[i],
            'w_pool_out': w_pool_out[i], 'w_o': w_o[i], 'w_up': w_up[i], 'ffn_conv_w': ffn_conv_w[i],
            'ffn_conv_b': ffn_conv_b[i], 'w_down': w_down[i],
        }
        mod_lat = (jax.nn.silu(c) @ w_ada[i] + b_ada[i])[:, None, :]
        mod_ctx = (jax.nn.silu(c_ctx) @ w_ada[i] + b_ada[i])[None, None, :]
        sh1, sc1, g1, sh2, sc2, g2 = jnp.split(mod_lat, 6, axis=-1)
        csh1, csc1, cg1, csh2, csc2, cg2 = jnp.split(mod_ctx, 6, axis=-1)

        h_ctx = modulate(x_ctx, norm1_w[i], csh1, csc1)
        if i == DEPTH - 1:
            h0_f, h0_b = context_ssd_states(h_ctx, lp)
        else:
            zero = _zero_state(x_ctx.shape[0])
            out_ctx, h0_f, h0_b = token_mixer(h_ctx, lp, zero, zero, grid=False)
            x_ctx = x_ctx + cg1 * out_ctx
            x_ctx = x_ctx + cg2 * conv_ffn(modulate(x_ctx, norm2_w[i], csh2, csc2), lp, grid=False)

        h_lat = modulate(x_lat, norm1_w[i], sh1, sc1)
        out_lat, _, _ = token_mixer(h_lat, lp, h0_f, h0_b, grid=True)
        x_lat = x_lat + g1 * out_lat
        x_lat = x_lat + g2 * conv_ffn(modulate(x_lat, norm2_w[i], sh2, sc2), lp, grid=True)
    return rms_norm(x_lat, final_norm_w)
```

```python
import numpy as np
from contextlib import ExitStack
import concourse.bass as bass
import concourse.mybir as mybir
from concourse.bass_utils import run_bass_kernel_spmd

F32, BF16 = mybir.dt.float32, mybir.dt.bfloat16
AF = mybir.ActivationFunctionType
ALU = mybir.AluOpType
AX = mybir.AxisListType

D = 2048
L = 4096
NCTX = 256
TMAIN = 2176
NCH_MAIN = 17
T2 = 2112
TOWN = 2048
HEADS = 32
DFF = 5632
IN_COLS = 12352
C_DT, C_XBC, C_Z, C_V, C_GS, C_GP = 0, 64, 4160, 6208, 8256, 10304
EPS = 1e-6


_FENCE = [{}]


class Buf:
    __slots__ = ("name", "w", "r", "excl", "f")

    def __init__(self, name, excl=False):
        self.name = name
        self.w = {}
        self.r = {}
        self.excl = excl
        self.f = _FENCE[0]


class Eng:
    def __init__(self, K, name, e):
        self.K, self.name, self.e = K, name, e
        self.sem = None
        self.cnt = 0
        self.waited = {}
        self.fence_done = None

    def next_ev(self):
        if self.sem is None or self.cnt >= 30000:
            self.sem = self.K.new_sem(f"e{self.name}")
            self.cnt = 0
        self.cnt += 1
        return (self.sem, self.cnt)


class Kern:
    def __init__(self, nc, es):
        self.nc, self.es = nc, es
        self.sems = []
        self.is_dma = []
        self.total = []
        self.pe = Eng(self, "pe", nc.tensor)
        self.act = Eng(self, "act", nc.scalar)
        self.dve = Eng(self, "dve", nc.vector)
        self.pool = Eng(self, "pool", nc.gpsimd)
        self.sp = Eng(self, "sp", nc.sync)
        self.engs = [self.pe, self.act, self.dve, self.pool, self.sp]
        self.dma_key = {}
        self.ninst = 0

    def new_sem(self, name, dma=False):
        h = self.es.enter_context(self.nc.semaphore(f"{name}_{len(self.sems)}"))
        self.sems.append(h)
        self.is_dma.append(dma)
        self.total.append(0)
        return len(self.sems) - 1

    def _waits(self, eng, reads, writes):
        deps = {}

        def add(d):
            for s, v in d.items():
                if deps.get(s, 0) < v:
                    deps[s] = v
        for b in list(reads) + list(writes):
            if b.f and eng.fence_done is not b.f:
                for s, v in b.f.items():
                    if eng.waited.get(s, 0) < v:
                        eng.e.wait_ge(self.sems[s], v)
                        eng.waited[s] = v
                        self.ninst += 1
                eng.fence_done = b.f
        for b in reads:
            add(b.w)
            if b.excl:
                add(b.r)
        for b in writes:
            add(b.w)
            add(b.r)
        for s, v in deps.items():
            if self.is_dma[s]:
                v = self.total[s]
            if eng.waited.get(s, 0) < v:
                eng.e.wait_ge(self.sems[s], v)
                eng.waited[s] = v
                self.ninst += 1

    def _record(self, ev, reads, writes):
        s, v = ev
        for b in reads:
            tgt = b.w if b.excl else b.r
            if tgt.get(s, 0) < v:
                tgt[s] = v
        for b in writes:
            if b.w.get(s, 0) < v:
                b.w[s] = v
            b.r = {}

    def op(self, eng, fn, reads=(), writes=()):
        self._waits(eng, reads, writes)
        inst = fn(eng.e)
        ev = eng.next_ev()
        inst.then_inc(self.sems[ev[0]], 1)
        self.total[ev[0]] = ev[1]
        self._record(ev, reads, writes)
        self.ninst += 1
        return ev

    def dma(self, q, out_ap, in_ap, key, reads=(), writes=()):
        s = self.dma_key.get(key)
        if s is None or self.total[s] > 30000:
            s = self.new_sem("d" + key, dma=True)
            self.dma_key[key] = s
        self._waits(q, reads, writes)
        inst = q.e.dma_start(out=out_ap, in_=in_ap)
        inst.then_inc(self.sems[s], 16)
        self.total[s] += 16
        self._record((s, self.total[s]), reads, writes)
        self.ninst += 1

    def fence(self):
        _FENCE[0] = {s: v for s, v in enumerate(self.total) if v > 0}

    def barrier(self):
        for eng in self.engs:
            for s in range(len(self.sems)):
                v = self.total[s]
                if v > 0 and eng.waited.get(s, 0) < v:
                    eng.e.wait_ge(self.sems[s], v)
                    eng.waited[s] = v


def bl(name, *shape):
    if len(shape) == 1:
        return [Buf(f"{name}{i}") for i in range(shape[0])]
    return [bl(f"{name}{i}_", *shape[1:]) for i in range(shape[0])]


def flat(x):
    out = []
    for e in x:
        if isinstance(e, list):
            out.extend(flat(e))
        else:
            out.append(e)
    return out


def build(stop_after=99, debug=False):
    nc = bass.Bass("TRN2", target_bir_lowering=False)
    okind = "ExternalOutput" if debug else "Internal"

    def din(name, shape, dt=F32):
        return nc.dram_tensor(name, list(shape), dt, kind="ExternalInput").ap()

    def dscr(name, shape, dt=F32):
        return nc.dram_tensor(name, list(shape), dt, kind=okind).ap()

    x_in = din("x", [L, D])
    ctx_in = din("ctx", [NCTX, D])
    cvec = din("cvec", [128, 16, 2])
    w_ada = din("w_ada", [D, 6 * D])
    b_ada_pp = din("b_ada_pp", [128, 96])
    b_ada_g = din("b_ada_g", [128, 2, D])
    n1w = din("n1w", [128, 16])
    n2w = din("n2w", [128, 16])
    w_in = din("w_in", [D, IN_COLS])
    convw = din("convw", [128, 32, 3])
    convb = din("convb", [128, 32])
    dtb = din("dtb", [128, 64])
    alog = din("alog", [128, 64])
    dsk = din("dsk", [128, 64])
    snw = din("snw", [128, 16])
    w_so = din("w_so", [D, D])
    poolw = din("poolw", [4, 512, 512])
    pscale = din("pscale", [128, 16])
    w_po = din("w_po", [D, D])
    w_o = din("w_o", [D, D])
    w_up = din("w_up", [D, 2 * DFF])
    fcw = din("fcw", [128, 44, 3])
    fcb = din("fcb", [128, 44])
    w_dn = din("w_dn", [DFF, D])
    fnw = din("fnw", [128, D])
    consts = din("consts", [128, 10, 128])
    out = nc.dram_tensor("out", [TOWN, D], F32, kind="ExternalOutput").ap()

    XBC1 = dscr("XBC1", [24, 128, NCTX + 1920], BF16)
    DT1 = dscr("DT1", [NCTX + 1921, 64])
    XBC = dscr("XBC", [32, 128, TMAIN], BF16)
    DTm = dscr("DTm", [TMAIN + 1, 64])
    ZS = dscr("ZS", [TMAIN + 1, D], BF16)
    VV = dscr("VV", [TMAIN + 1, D], BF16)
    GS = dscr("GS", [16, 128, TMAIN + 1], BF16)
    GP = dscr("GP", [16, 128, TMAIN + 1], BF16)
    HB = dscr("HB", [NCH_MAIN, 128, D], BF16)
    YS = dscr("YS", [16, 128, TMAIN], BF16)
    YP = dscr("YP", [16, 128, T2], BF16)
    X2 = dscr("X2", [T2, D])
    X3 = dscr("X3", [TOWN, D])
    GBS = dscr("GBS", [128, 2, D])

    _uid = [0]

    def SBT(name, shape, dt):
        _uid[0] += 1
        return nc.sbuf_tensor(f"{name}_u{_uid[0]}", list(shape), dt)

    es = ExitStack()
    K = Kern(nc, es)
    pe, act, dve, pool, sp = K.pe, K.act, K.dve, K.pool, K.sp

    def sb(name, shape, dt):
        return es.enter_context(SBT(name, list(shape), dt))

    PS = [es.enter_context(nc.psum_tensor(f"ps{i}", [128, 512], F32)) for i in range(8)]
    PSB = [Buf(f"psb{i}", excl=True) for i in range(8)]

    cst = sb("cst", [128, 10, 128], F32)
    cstb = sb("cstb", [128, 10, 128], BF16)
    B_cst = Buf("cst")
    K.dma(sp, cst[:], consts, "cst", writes=[B_cst])
    K.dma(pool, cstb[:], consts, "cstb", writes=[B_cst])
    IDb = cstb[:, 0, :]
    LE_f, GE_f, GT_f, LT_f, ONES_f = cst[:, 1, :], cst[:, 2, :], cst[:, 3, :], cst[:, 4, :], cst[:, 9, :]
    LE_b, GE_b = cstb[:, 1, :], cstb[:, 2, :]

    small = {}
    B_small = Buf("small")
    for nm, src, shp in [("n1w", n1w, [128, 16]), ("n2w", n2w, [128, 16]), ("convw", convw, [128, 32, 3]),
                         ("convb", convb, [128, 32]), ("dtb", dtb, [128, 64]), ("alog", alog, [128, 64]),
                         ("dsk", dsk, [128, 64]), ("snw", snw, [128, 16]), ("pscale", pscale, [128, 16]),
                         ("fcw", fcw, [128, 44, 3]), ("fcb", fcb, [128, 44]), ("bpp", b_ada_pp, [128, 96]),
                         ("cvec", cvec, [128, 16, 2])]:
        t = sb("sm_" + nm, shp, F32)
        K.dma(sp, t[:], src, "small", writes=[B_small])
        small[nm] = t

    mod = sb("mod", [128, 96, 2], F32)
    gm1 = sb("gm1", [128, 16], F32)
    cgm1 = sb("cgm1", [128, 16], F32)
    gm2 = sb("gm2", [128, 16], F32)
    Aneg = sb("Aneg", [128, 64], F32)
    Dsum = sb("Dsum", [128, 32], F32)
    DI = sb("DI", [128, 32, 128], BF16)
    B_mod = Buf("mod")
    B_gb = Buf("gb")
    B_ssdc = Buf("ssdc")

    def silu_c(stack, tag):
        csf = stack.enter_context(SBT(f"csf{tag}", [128, 16, 2], F32))
        csil = stack.enter_context(SBT(f"csil{tag}", [128, 16, 2], BF16))
        crep = stack.enter_context(SBT(f"crep{tag}", [128, 16, 128], BF16))
        B_c = Buf("csil")
        K.op(act, lambda e: e.activation(out=csf[:], in_=small["cvec"][:], func=AF.Silu), [B_small], [B_c])
        K.op(dve, lambda e: e.tensor_copy(csil[:], csf[:]), [B_c], [B_c])
        K.op(dve, lambda e: e.tensor_copy(crep[:], csf[:, :, 0:1].to_broadcast([128, 16, 128])), [B_c], [B_c])
        return csil, crep, B_c

    def mod_group(grp, wa_t, B_wa_t, csil, B_c):
        bank = grp % 2
        for j4 in range(4):
            j = grp * 4 + j4

            def f(e, j4=j4, bank=bank):
                for kc in range(16):
                    i = e.matmul(PS[bank][:, j4 * 2:j4 * 2 + 2], wa_t[:, kc, j4 * 128:(j4 + 1) * 128],
                                 csil[:, kc, :], start=(kc == 0), stop=(kc == 15))
                return i
            K.op(pe, f, [B_wa_t, B_c], [PSB[bank]])
            yield
            K.op(dve, lambda e, j=j, j4=j4, bank=bank: e.tensor_scalar(
                mod[:, j, :], PS[bank][:, j4 * 2:j4 * 2 + 2], small["bpp"][:, j:j + 1], None, ALU.add),
                [PSB[bank], B_small], [B_mod])
            yield

    def derive(dst, nw, col, which):
        K.op(dve, lambda e: e.tensor_scalar(dst[:], mod[:, col:col + 16, which], 1.0, None, ALU.add), [B_mod], [B_mod])
        K.op(dve, lambda e: e.tensor_tensor(dst[:], dst[:], small[nw][:], ALU.mult), [B_mod, B_small], [B_mod])

    with ExitStack() as ph:
        def psb(name, shape, dt):
            return ph.enter_context(SBT(name, list(shape), dt))
        csil, crep, B_c = silu_c(ph, "a")
        wa = [psb(f"wa{i}", [128, 16, 512], BF16) for i in range(2)]
        B_wa = [Buf("wa0"), Buf("wa1")]
        for grp in range(8):
            s = grp % 2
            K.dma(pool, wa[s][:], w_ada[:, grp * 512:(grp + 1) * 512].rearrange("(kc p) n -> p kc n", p=128),
                  f"wa{s}", writes=[B_wa[s]])
            for _ in mod_group(grp, wa[s], B_wa[s], csil, B_c):
                pass
        derive(gm1, "n1w", 16, 0)
        derive(cgm1, "n1w", 16, 1)
        K.op(act, lambda e: e.activation(out=Aneg[:], in_=small["alog"][:], func=AF.Exp), [B_small], [B_ssdc])
        K.op(dve, lambda e: e.tensor_scalar(Aneg[:], Aneg[:], -1.0, None, ALU.mult), [B_ssdc], [B_ssdc])
        K.op(dve, lambda e: e.tensor_tensor(Dsum[:], small["dsk"][:, 0:32], small["dsk"][:, 32:64], ALU.add),
             [B_small], [B_ssdc])
        K.op(dve, lambda e: e.tensor_tensor(DI[:], cst[:, 0:1, :].to_broadcast([128, 32, 128]),
                                            Dsum[:].unsqueeze(2).to_broadcast([128, 32, 128]), ALU.mult),
             [B_ssdc, B_cst], [B_ssdc])
        K.barrier()

    def p0b(stack):
        wa2 = [stack.enter_context(SBT(f"wb{i}", [128, 16, 512], BF16)) for i in range(2)]
        bsl = [stack.enter_context(SBT(f"bsl{i}", [128, 512], F32)) for i in range(2)]
        gsl = [stack.enter_context(SBT(f"gsl{i}", [128, 512], F32)) for i in range(2)]
        B_wa2 = [Buf("wb0"), Buf("wb1")]
        Bbsl = [Buf("bsl0"), Buf("bsl1")]
        Bgsl = [Buf("gsl0"), Buf("gsl1")]
        csil, crep, B_c = silu_c(stack, "b")

        def gen():
            ng = 0
            for grp in range(8, 24):
                s = grp % 2
                K.dma(pool, wa2[s][:], w_ada[:, grp * 512:(grp + 1) * 512].rearrange("(kc p) n -> p kc n", p=128),
                      f"wb{s}", writes=[B_wa2[s]])
                yield
                yield from mod_group(grp, wa2[s], B_wa2[s], csil, B_c)
                gi = {8: 0, 9: 0, 10: 0, 11: 0, 20: 1, 21: 1, 22: 1, 23: 1}.get(grp)
                if gi is not None:
                    c0 = (grp % 4) * 512
                    bk = 2 + grp % 2
                    r = ng % 2
                    ng += 1
                    K.dma(sp, bsl[r][:], b_ada_g[:, gi, c0:c0 + 512], f"bsl{r}", writes=[Bbsl[r]])

                    def f2(e, s=s, bk=bk):
                        for kc in range(16):
                            i = e.matmul(PS[bk][:, :], crep[:, kc, :], wa2[s][:, kc, :], start=(kc == 0), stop=(kc == 15))
                        return i
                    K.op(pe, f2, [B_wa2[s], B_c], [PSB[bk]])
                    yield
                    K.op(dve, lambda e, bk=bk, r=r: e.tensor_tensor(gsl[r][:], PS[bk][:, :], bsl[r][:], ALU.add),
                         [PSB[bk], Bbsl[r]], [Bgsl[r]])
                    yield
                    K.dma(sp, GBS[:, gi, c0:c0 + 512], gsl[r][:], f"gsl{r}", reads=[Bgsl[r]])
                    yield
            derive(gm2, "n2w", 64, 0)
            yield
        return gen()
    sh1 = mod[:, 0:16, 0]
    csh1 = mod[:, 0:16, 1]
    sh2 = mod[:, 48:64, 0]
    if stop_after <= 0:
        return finish(nc, K, es, out, sp)

    def _interleave(*gens):
        gens = [g for g in gens if g is not None]
        while gens:
            for g in list(gens):
                try:
                    next(g)
                except StopIteration:
                    gens.remove(g)

    def norm_mod_T(ph, src_rows, gm, shf, HT, BHT, col0, tag):
        NS = 3
        xt = [ph.enter_context(SBT(f"nx{tag}{i}", [128, D], F32)) for i in range(NS)]
        xn = [ph.enter_context(SBT(f"nn{tag}{i}", [128, D], BF16)) for i in range(2)]
        junk = ph.enter_context(SBT(f"nj{tag}", [128, D], BF16))
        st = [ph.enter_context(SBT(f"ns{tag}{i}", [128, 2], F32)) for i in range(NS)]
        Bx = [Buf(f"nx{i}") for i in range(NS)]
        Bn = [Buf("nn0"), Buf("nn1")]
        Bs = [Buf(f"ns{i}") for i in range(NS)]
        Bj = Buf("nj")
        mtmp = [ph.enter_context(SBT(f"nm{tag}{i}", [128, 4, 128], F32)) for i in range(2)]
        Bmt = [Buf("nm0"), Buf("nm1")]
        n_it = len(src_rows)
        cols = []
        c = col0
        for src in src_rows:
            cols.append(c)
            c += src.shape[0]

        def load(i):
            n = src_rows[i].shape[0]
            K.dma(sp, xt[i % NS][0:n, :], src_rows[i], f"nx{i % NS}", writes=[Bx[i % NS]])

        def stageA(i):
            n = src_rows[i].shape[0]
            s = i % NS
            K.op(dve, lambda e: e.memset(st[s][:, :], 0.0), [], [Bs[s]])
            yield
            K.op(act, lambda e: e.activation(out=junk[0:n, :], in_=xt[s][0:n, :], func=AF.Square,
                                             accum_out=st[s][0:n, 0:1]), [Bx[s]], [Bj, Bs[s]])
            yield
            K.op(dve, lambda e: e.tensor_scalar(st[s][0:n, 1:2], st[s][0:n, 0:1], 1.0 / D, EPS, ALU.mult, ALU.add),
                 [Bs[s]], [Bs[s]])
            yield
            K.op(act, lambda e: e.activation(out=st[s][0:n, 1:2], in_=st[s][0:n, 1:2], func=AF.Sqrt), [Bs[s]], [Bs[s]])
            yield
            K.op(dve, lambda e: e.reciprocal(st[s][0:n, 1:2], st[s][0:n, 1:2]), [Bs[s]], [Bs[s]])
            yield

        def stageB(i):
            n = src_rows[i].shape[0]
            s = i % NS
            s2 = i % 2
            c = cols[i]
            K.op(act, lambda e: e.activation(out=xn[s2][0:n, :], in_=xt[s][0:n, :], func=AF.Copy, scale=st[s][0:n, 1:2]),
                 [Bx[s], Bs[s]], [Bn[s2]])
            yield
            for q in range(4):
                bank = (i * 4 + q) % 4

                def f(e, q=q, bank=bank):
                    for b4 in range(4):
                        blk = q * 4 + b4
                        i_ = e.matmul(PS[bank][:, b4 * 128:b4 * 128 + n], xn[s2][0:n, blk * 128:(blk + 1) * 128],
                                      IDb[0:n, 0:n], start=True, stop=True)
                    return i_
                K.op(pe, f, [Bn[s2], B_cst], [PSB[bank]])
                yield
                tb = (i * 4 + q) % 2
                psv = PS[bank][:, :].rearrange("p (a b) -> p a b", a=4)[:, :, 0:n]
                K.op(dve, lambda e, q=q, tb=tb, psv=psv: e.tensor_tensor(
                    mtmp[tb][:, :, 0:n], psv, gm[:, q * 4:(q + 1) * 4].unsqueeze(2).to_broadcast([128, 4, n]), ALU.mult),
                    [PSB[bank], B_mod], [Bmt[tb]])
                yield
                K.op(dve, lambda e, q=q, tb=tb: e.tensor_tensor(
                    HT[:, q * 4:(q + 1) * 4, c:c + n], mtmp[tb][:, :, 0:n],
                    shf[:, q * 4:(q + 1) * 4].unsqueeze(2).to_broadcast([128, 4, n]), ALU.add),
                    [Bmt[tb], B_mod], [BHT])
                yield
        load(0)
        if n_it > 1:
            load(1)
        for _ in stageA(0):
            pass
        for i in range(n_it):
            if i + 2 < n_it:
                load(i + 2)
            _interleave(stageB(i), stageA(i + 1) if i + 1 < n_it else None)

    def fm_gemm(ph, W, col0, nblk, KC, act_fn, tiles, epi, tag, wbufs=None, wset=None):
        if wset is not None:
            wg, Bw = wset
        else:
            wg = [ph.enter_context(SBT(f"wg{tag}{i}", [128, KC, 512], BF16)) for i in range(2)]
            Bw = [Buf("wg0"), Buf("wg1")]
        ngrp = (nblk + 3) // 4
        cnt = 0
        for g in range(ngrp):
            s = g % 2
            nb = min(4, nblk - g * 4)
            K.dma(pool, wg[s][:, :, 0:nb * 128],
                  W[0:KC * 128, col0 + g * 512:col0 + g * 512 + nb * 128].rearrange("(kc p) n -> p kc n", p=128),
                  f"wg{tag}{s}", writes=[Bw[s]])
            for jb in range(nb):
                j = g * 4 + jb
                for ti, (t0, n) in enumerate(tiles):
                    bank = 4 + cnt % 4
                    cnt += 1
                    aps = [act_fn(kc, t0, n) for kc in range(KC)]
                    rb = [Bw[s]] + flat([a[1] for a in aps])

                    def f(e, s=s, jb=jb, aps=aps, bank=bank, n=n):
                        for kc in range(KC):
                            i_ = e.matmul(PS[bank][:, 0:n], wg[s][:, kc, jb * 128:(jb + 1) * 128], aps[kc][0],
                                          start=(kc == 0), stop=(kc == KC - 1))
                        return i_
                    K.op(pe, f, rb, [PSB[bank]])
                    epi(j, ti, t0, n, bank)
                    yield

    def tm_gemm(ph, W, col0, ncols, KC, act_fn, chunks, epi, tag, row0=0, nbuf=2, cw=512, pre=None, wset=None):
        if wset is not None:
            wt, Bw = wset
        else:
            wt = [ph.enter_context(SBT(f"wt{tag}{i}", [128, KC, cw], BF16)) for i in range(nbuf)]
            Bw = [Buf(f"wt{i}") for i in range(nbuf)]
        nct = (ncols + cw - 1) // cw
        cnt = 0
        its = [(ct_, c_) for ct_ in range(nct) for c_ in chunks]
        if pre is not None:
            pre(0, its[0][0], its[0][1])
        for ct in range(nct):
            s = ct % nbuf
            w = min(cw, ncols - ct * cw)
            K.dma(pool, wt[s][:, :, 0:w],
                  W[row0:row0 + KC * 128, col0 + ct * cw:col0 + ct * cw + w].rearrange("(kc p) n -> p kc n", p=128),
                  f"wt{tag}{s}", writes=[Bw[s]])
            for c in chunks:
                bank = 4 + cnt % 4
                cnt += 1
                aps = [act_fn(kc, c) for kc in range(KC)]
                n = aps[0][2]
                rb = [Bw[s]] + flat([a[1] for a in aps])

                def f(e, s=s, aps=aps, bank=bank, n=n, w=w):
                    for kc in range(KC):
                        i_ = e.matmul(PS[bank][0:n, 0:w], aps[kc][0], wt[s][:, kc, 0:w],
                                      start=(kc == 0), stop=(kc == KC - 1))
                    return i_
                K.op(pe, f, rb, [PSB[bank]])
                if pre is not None:
                    if cnt < len(its):
                        pre(cnt, its[cnt][0], its[cnt][1])
                    epi(cnt - 1, ct, c, n, w, bank)
                else:
                    epi(ct, c, n, w, bank)
                yield

    def run(g):
        for _ in g:
            pass

    def interleave_gen(*gens):
        gens = [g for g in gens if g is not None]
        while gens:
            for g in list(gens):
                try:
                    next(g)
                    yield
                except StopIteration:
                    gens.remove(g)

    def interleave(*gens):
        run(interleave_gen(*gens))

    def weave(main, side, ratio):
        acc = 0.0
        done = False
        for _ in main:
            acc += ratio
            while acc >= 1.0 and not done:
                acc -= 1.0
                try:
                    next(side)
                except StopIteration:
                    done = True
        if not done:
            run(side)

    def chain(*gens):
        for g in gens:
            yield from g

    def v3(ap, h):
        return ap.rearrange("p (h q) -> p h q", h=h)

    class WB:
        pass

    def mk_wb(stack, tag, nx=2):
        wb = WB()
        wb.nx = nx

        def T(n, shp, dt):
            return stack.enter_context(SBT(f"{n}{tag}", shp, dt))
        wb.xin = [T(f"xin{i}", [128, 32, 128], BF16) for i in range(nx)]
        wb.dtt = [T(f"dtt{i}", [128, 64], F32) for i in range(nx)]
        wb.Bxin = [Buf(f"xin{i}") for i in range(nx)]
        wb.Bdtt = [Buf(f"dtt{i}") for i in range(nx)]
        wb.dtA_ = [T(f"dtA{i}", [128, 64], F32) for i in range(2)]
        wb.dth_ = [T(f"dth{i}", [128, 64], BF16) for i in range(2)]
        wb.lnd_ = [T(f"lnd{i}", [128, 64], F32) for i in range(2)]
        wb.sm_ = [T(f"sm{i}", [128, 5, 64], F32) for i in range(2)]
        wb.Bsm_ = [Buf("sm0"), Buf("sm1")]
        wb.xsT_ = [T(f"xsT{i}", [128, D], BF16) for i in range(2)]
        wb.BT_ = [T(f"BT{i}", [128, 1024], BF16) for i in range(2)]
        wb.BxsT_, wb.BBT_ = [Buf("xsT0"), Buf("xsT1")], [Buf("BT0"), Buf("BT1")]
        wb.xw_ = [T(f"xw{i}", [128, D], BF16) for i in range(2)]
        wb.Bxw_ = [Buf("xw0"), Buf("xw1")]
        return wb

    SS = WB()

    def load_chunk(wb, slot, xsrc, nblk, dsrc):
        K.dma(sp, wb.xin[slot][:, 0:nblk, :], xsrc, f"xin{slot}", writes=[wb.Bxin[slot]])
        K.dma(sp, wb.dtt[slot][:], dsrc, f"dtt{slot}", writes=[wb.Bdtt[slot]])

    def prep_small(wb, slot, full=False, bank=7, xs=None):
        xs = slot if xs is None else xs
        dtA, sm, Bsm, dth, lnd, dtt, Bdtt = (wb.dtA_[slot], wb.sm_[slot], wb.Bsm_[slot], wb.dth_[slot], wb.lnd_[slot],
                                             wb.dtt[xs], wb.Bdtt[xs])
        K.op(dve, lambda e: e.tensor_tensor(dtA[:], dtt[:], Aneg[:], ALU.mult), [Bdtt, B_ssdc], [Bsm])
        yield

        def f(e):
            e.matmul(PS[bank][:, 0:32], LE_f, dtA[:, 0:32], start=True, stop=True)
            e.matmul(PS[bank][:, 32:64], GE_f, dtA[:, 32:64], start=True, stop=True)
            return e.matmul(PS[bank][:, 64:128], ONES_f, dtA[:, 0:64], start=True, stop=True)
        K.op(pe, f, [Bsm, B_cst], [PSB[bank]])
        yield
        if full:
            K.op(dve, lambda e: e.tensor_copy(dth[:], dtA[:]), [Bsm], [Bsm])
            yield
            K.op(act, lambda e: e.activation(out=lnd[:], in_=dtt[:], func=AF.Ln), [Bdtt], [Bsm])
            yield
        K.op(dve, lambda e: e.tensor_copy(sm[:, 0:2, :], PS[bank][:, 0:128].rearrange("p (a b) -> p a b", a=2)),
             [PSB[bank]], [Bsm])
        yield
        K.op(act, lambda e: e.activation(out=sm[:, 2:4, :], in_=sm[:, 0:2, :], func=AF.Exp), [Bsm], [Bsm])
        yield
        K.op(dve, lambda e: e.tensor_tensor(sm[:, 4, :], sm[:, 1, :], sm[:, 0, :], ALU.subtract), [Bsm], [Bsm])
        yield
        K.op(act, lambda e: e.activation(out=sm[:, 4, :], in_=sm[:, 4, :], func=AF.Exp), [Bsm], [Bsm])
        yield
        K.op(dve, lambda e: e.tensor_tensor(sm[:, 4, :], sm[:, 4, :], dtt[:], ALU.mult), [Bsm, Bdtt], [Bsm])
        yield

    def prep_tm(wb, slot, banks=(0, 1), xs=None):
        xs = slot if xs is None else xs
        xsT, BT, BxsT, BBT, xin, Bxin = wb.xsT_[slot], wb.BT_[slot], wb.BxsT_[slot], wb.BBT_[slot], wb.xin[xs], wb.Bxin[xs]
        for q in range(6):
            bank = banks[q % len(banks)]

            def f(e, q=q, bank=bank):
                for b4 in range(4):
                    i_ = e.matmul(PS[bank][:, b4 * 128:(b4 + 1) * 128], xin[:, q * 4 + b4, :], IDb, start=True, stop=True)
                return i_
            K.op(pe, f, [Bxin, B_cst], [PSB[bank]])
            yield
            if q < 4:
                K.op(act, lambda e, q=q, bank=bank: e.activation(out=xsT[:, q * 512:(q + 1) * 512], in_=PS[bank][:, :],
                                                                 func=AF.Copy), [PSB[bank]], [BxsT])
            else:
                K.op(act, lambda e, q=q, bank=bank: e.activation(out=BT[:, (q - 4) * 512:(q - 3) * 512],
                                                                 in_=PS[bank][:, :], func=AF.Copy), [PSB[bank]], [BBT])
            yield

    def state_step(wb, d_, slot, need_bf=True, banks=(4, 5), xw_eng=None):
        S, Sbf, BS, BSbf = SS.S, SS.Sbf, SS.BS, SS.BSbf
        sm, Bsm, xsT, BT, BxsT, BBT, xw, Bxw = (wb.sm_[slot], wb.Bsm_[slot], wb.xsT_[slot], wb.BT_[slot], wb.BxsT_[slot],
                                                wb.BBT_[slot], wb.xw_[slot], wb.Bxw_[slot])
        w = sm[:, 4, d_ * 32:(d_ + 1) * 32]
        cd = sm[:, 3, d_ * 32:(d_ + 1) * 32]
        K.op(xw_eng or pool, lambda e: e.tensor_tensor(v3(xw[:, :], 32), v3(xsT[:, :], 32),
                                                       w.unsqueeze(2).to_broadcast([128, 32, 64]), ALU.mult),
             [BxsT, Bsm], [Bxw])
        yield
        for qd in range(4):
            bank = banks[qd % len(banks)]

            def f(e, qd=qd, bank=bank):
                for g in (2 * qd, 2 * qd + 1):
                    i_ = e.matmul(PS[bank][:, (g % 2) * 256:(g % 2 + 1) * 256], BT[:, g * 128:(g + 1) * 128],
                                  xw[:, g * 256:(g + 1) * 256], start=True, stop=True)
                return i_
            K.op(pe, f, [BBT, Bxw], [PSB[bank]])
            yield
            sl = slice(qd * 512, (qd + 1) * 512)
            K.op(dve, lambda e, sl=sl, qd=qd: e.tensor_tensor(
                v3(S[d_][:, sl], 8), v3(S[d_][:, sl], 8),
                cd[:, qd * 8:(qd + 1) * 8].unsqueeze(2).to_broadcast([128, 8, 64]), ALU.mult), [BS[d_], Bsm], [BS[d_]])
            yield
            K.op(dve, lambda e, sl=sl, bank=bank: e.tensor_tensor(S[d_][:, sl], S[d_][:, sl], PS[bank][:, :], ALU.add),
                 [BS[d_], PSB[bank]], [BS[d_]])
            yield
        if need_bf:
            K.op(act, lambda e: e.activation(out=Sbf[d_][:], in_=S[d_][:], func=AF.Copy), [BS[d_]], [BSbf[d_]])
            yield

    def x1src(col):
        return XBC1[:, :, col:col + 128].rearrange("j p t -> p j t")

    def xmsrc(c):
        return XBC[:, :, c * 128:(c + 1) * 128].rearrange("j p t -> p j t")

    def state_sweep(wb, pbank, tbanks, sbanks, xw_eng):
        sched = [(0, x1src(0), DT1[0:128, :], None), (0, x1src(128), DT1[128:256, :], None),
                 (1, x1src(128), DT1[128:256, :], None), (1, x1src(0), DT1[0:128, :], None)]
        for gc in range(31, 16, -1):
            col = NCTX + (gc - 17) * 128
            r0 = NCTX + gc * 128 - (TMAIN - 1)
            sched.append((1, x1src(col), DT1[r0:r0 + 128, :], None))
        for c in range(16, 0, -1):
            sched.append((1, xmsrc(c)[:, 0:24, :], DTm[c * 128:(c + 1) * 128, :], c))
        nst = len(sched)
        nx = wb.nx
        for k in range(min(nx - 1, nst)):
            load_chunk(wb, k % nx, sched[k][1], 24, sched[k][2])
        yield from chain(prep_small(wb, 0, bank=pbank, xs=0), prep_tm(wb, 0, tbanks, xs=0))
        for i, (d_, xs_, ds_, hbc) in enumerate(sched):
            slot = i % 2
            nxt = None
            k = i + nx - 1
            if k < nst:
                load_chunk(wb, k % nx, sched[k][1], 24, sched[k][2])
            if i + 1 < nst:
                nxt = chain(prep_small(wb, (i + 1) % 2, bank=pbank, xs=(i + 1) % nx),
                            prep_tm(wb, (i + 1) % 2, tbanks, xs=(i + 1) % nx))
            if hbc is not None:
                K.dma(sp, HB[hbc], SS.Sbf[1][:], "hb", reads=[SS.BSbf[1]], writes=[SS.B_HB])
            need_bf = (i == 1) or (i + 1 < nst and sched[i + 1][3] is not None) or (i == nst - 1)
            yield from interleave_gen(state_step(wb, d_, slot, need_bf=need_bf, banks=sbanks, xw_eng=xw_eng), nxt)
        K.dma(sp, HB[0], SS.Sbf[1][:], "hb", reads=[SS.BSbf[1]], writes=[SS.B_HB])

    def proj_phase(segs, full, side_factory=None, side_ratio=1.0):
        with ExitStack() as ph:
            Ttot = sum(s["T"] for s in segs)
            HT = ph.enter_context(SBT("HT", [128, 16, Ttot], BF16))
            BHT = Buf("HT")
            wsh = [ph.enter_context(SBT(f"wsh{i}", [128, 16, 512], BF16)) for i in range(2)]
            Bwsh = [Buf("wsh0"), Buf("wsh1")]
            c0 = 0
            for si, s in enumerate(segs):
                s["c0"] = c0
                with ExitStack() as ph2:
                    norm_mod_T(ph2, s["rows"], s["gm"], s["sh"], HT, BHT, c0, f"s{si}")
                    K.fence()
                c0 += s["T"]
            if debug:
                HTD = dscr(f"HTD{_uid[0]}", [128, 16, Ttot], BF16)
                K.dma(sp, HTD, HT[:], "dbg", reads=[BHT])
            nxb = 32 if full else 24
            side_stack = ExitStack()
            side = side_factory(side_stack) if side_factory is not None else None
            for si, s in enumerate(segs):
                T = s["T"]
                sub = ExitStack()
                raw = [sub.enter_context(SBT(f"raw{si}_{i}", [128, T + 2], BF16)) for i in range(2)]
                acc = sub.enter_context(SBT(f"acc{si}", [128, T], F32))
                xo = [sub.enter_context(SBT(f"xo{si}_{i}", [128, T], BF16)) for i in range(2)]
                Braw = [Buf("raw0"), Buf("raw1")]
                Bacc = Buf("acc")
                Bxo = [Buf("xo0"), Buf("xo1")]
                for i in range(2):
                    K.op(dve, lambda e, i=i: e.memset(raw[i][:], 0.0), [], [Braw[i]])
                tiles = [(t0, min(512, T - t0)) for t0 in range(0, T, 512)]
                lo, hi = s["out_lo"], s["out_hi"]
                no = hi - lo
                cw, cb = small["convw"], small["convb"]
                c0s = s["c0"]

                def epi(j, ti, t0, n, bank, raw=raw, Braw=Braw, tiles=tiles, lo=lo, hi=hi, no=no, acc=acc, Bacc=Bacc,
                        xo=xo, Bxo=Bxo, s=s):
                    r = j % 2
                    K.op(act, lambda e: e.activation(out=raw[r][:, 1 + t0:1 + t0 + n], in_=PS[bank][:, 0:n],
                                                     func=AF.Copy), [PSB[bank]], [Braw[r]])
                    if ti == len(tiles) - 1:
                        K.op(dve, lambda e: e.tensor_scalar(acc[:, 0:no], raw[r][:, lo + 1:hi + 1], cw[:, j, 1:2],
                                                            cb[:, j:j + 1], ALU.mult, ALU.add),
                             [Braw[r], B_small], [Bacc])
                        K.op(dve, lambda e: e.scalar_tensor_tensor(acc[:, 0:no], raw[r][:, lo:hi], cw[:, j, 0:1],
                                                                   acc[:, 0:no], ALU.mult, ALU.add),
                             [Braw[r], B_small, Bacc], [Bacc])
                        K.op(dve, lambda e: e.scalar_tensor_tensor(acc[:, 0:no], raw[r][:, lo + 2:hi + 2],
                                                                   cw[:, j, 2:3], acc[:, 0:no], ALU.mult, ALU.add),
                             [Braw[r], B_small, Bacc], [Bacc])
                        K.op(act, lambda e: e.activation(out=xo[r][:, 0:no], in_=acc[:, 0:no], func=AF.Silu),
                             [Bacc], [Bxo[r]])
                        K.dma(sp, s["xbc_dst"](j), xo[r][:, 0:no], f"xo{r}", reads=[Bxo[r]])
                gx = fm_gemm(sub, w_in, C_XBC, nxb, 16, lambda kc, t0, n, c0s=c0s: (HT[:, kc, c0s + t0:c0s + t0 + n], [BHT]),
                             tiles, epi, f"x{si}", wset=(wsh, Bwsh))
                if side is not None and si == len(segs) - 1:
                    weave(gx, side, side_ratio)
                else:
                    run(gx)
                K.fence()
                sub.close()
            side_stack.close()
            sub = ExitStack()
            sp_t = [sub.enter_context(SBT(f"spt{i}", [128, 64], F32)) for i in range(4)]
            dto = [sub.enter_context(SBT(f"dto{i}", [128, 64], F32)) for i in range(2)]
            Bspt = Buf("spt")
            Bdto = [Buf("dto0"), Buf("dto1")]
            allch = []
            for s in segs:
                T = s["T"]
                for cix, t0 in enumerate(range(0, T, 128)):
                    allch.append((s, t0, min(128, T - t0)))
            cntd = [0]

            def epi_dt(ct, c, n, w, bank):
                s, t0, _ = c
                r = cntd[0] % 2
                cntd[0] += 1
                t, a, ex, ln = sp_t
                K.op(dve, lambda e: e.tensor_tensor(t[0:n, :], PS[bank][0:n, 0:64], small["dtb"][0:n, :], ALU.add),
                     [PSB[bank], B_small], [Bspt])
                K.op(act, lambda e: e.activation(out=a[0:n, :], in_=t[0:n, :], func=AF.Abs), [Bspt], [Bspt])
                K.op(act, lambda e: e.activation(out=ex[0:n, :], in_=a[0:n, :], func=AF.Exp, scale=-1.0), [Bspt], [Bspt])
                K.op(act, lambda e: e.activation(out=ln[0:n, :], in_=ex[0:n, :], func=AF.Ln, bias=1.0), [Bspt], [Bspt])
                K.op(dve, lambda e: e.scalar_tensor_tensor(dto[r][0:n, :], t[0:n, :], 0.0, ln[0:n, :], ALU.max, ALU.add),
                     [Bspt], [Bdto[r]])
                K.dma(sp, s["dt_dst"](t0, n), dto[r][0:n, :], f"dto{r}", reads=[Bdto[r]])
            run(tm_gemm(sub, w_in, C_DT, 64, 16,
                    lambda kc, c: (HT[:, kc, c[0]["c0"] + c[1]:c[0]["c0"] + c[1] + c[2]], [BHT], c[2]),
                    allch, epi_dt, "dt", wset=(wsh, Bwsh)))
            K.fence()
            sub.close()
            if full:
                s = segs[0]
                mch = [(s, t0, min(128, s["T"] - t0)) for t0 in range(0, s["T"], 128)]
                sw = ExitStack()
                wbs = mk_wb(sw, "s", nx=3)
                sub = ExitStack()
                zo = [sub.enter_context(SBT(f"zo{i}", [128, 512], BF16)) for i in range(2)]
                Bzo = [Buf("zo0"), Buf("zo1")]
                cz = [0]

                def mk_epi(dst, fn):
                    def epi_z(ct, c, n, w, bank):
                        _, t0, _ = c
                        r = cz[0] % 2
                        cz[0] += 1
                        K.op(act, lambda e: e.activation(out=zo[r][0:n, 0:w], in_=PS[bank][0:n, 0:w], func=fn),
                             [PSB[bank]], [Bzo[r]])
                        K.dma(sp, dst[t0:t0 + n, ct * 512:ct * 512 + w], zo[r][0:n, 0:w], f"zo{r}", reads=[Bzo[r]])
                    return epi_z
                af = lambda kc, c: (HT[:, kc, c[1]:c[1] + c[2]], [BHT], c[2])
                T = s["T"]
                tiles = [(t0, min(512, T - t0)) for t0 in range(0, T, 512)]

                def epi_g(j, ti, t0, n, bank):
                    r = cz[0] % 2
                    cz[0] += 1
                    K.op(act, lambda e: e.activation(out=zo[r][:, 0:n], in_=PS[bank][:, 0:n], func=AF.Sigmoid),
                         [PSB[bank]], [Bzo[r]])
                    dst = GS if j < 16 else GP
                    K.dma(sp, dst[j % 16, :, t0:t0 + n], zo[r][:, 0:n], f"zo{r}", reads=[Bzo[r]])
                main = chain(
                    tm_gemm(None, w_in, C_Z, D, 16, af, mch, mk_epi(ZS, AF.Silu), "z", wset=(wsh, Bwsh)),
                    tm_gemm(None, w_in, C_V, D, 16, af, mch, mk_epi(VV, AF.Copy), "v", wset=(wsh, Bwsh)),
                    fm_gemm(None, w_in, C_GS, 32, 16, lambda kc, t0, n: (HT[:, kc, t0:t0 + n], [BHT]), tiles, epi_g, "g",
                            wset=(wsh, Bwsh)))
                weave(main, state_sweep(wbs, 3, (0, 1), (2, 3), dve), 2.7)
                K.barrier()
                sub.close()
                sw.close()
            K.barrier()

    def rows_of(src, t0, t1):
        return [src[t:min(t + 128, t1), :] for t in range(t0, t1, 128)]

    proj_phase([
        dict(rows=rows_of(ctx_in, 0, NCTX), T=NCTX, gm=cgm1, sh=csh1, out_lo=0, out_hi=NCTX,
             xbc_dst=lambda j: XBC1[j, :, 0:NCTX], dt_dst=lambda t0, n: DT1[t0:t0 + n, :]),
        dict(rows=rows_of(x_in, TMAIN - 1, L), T=L - TMAIN + 1, gm=gm1, sh=sh1, out_lo=1, out_hi=L - TMAIN + 1,
             xbc_dst=lambda j: XBC1[j, :, NCTX:NCTX + L - TMAIN], dt_dst=lambda t0, n: DT1[NCTX + t0:NCTX + t0 + n, :]),
    ], full=False, side_factory=p0b, side_ratio=1.8)
    if stop_after <= 1:
        return finish(nc, K, es, out, sp)
    es_ssd = ExitStack()
    SS.S = [es_ssd.enter_context(SBT("S_f", [128, D], F32)), es_ssd.enter_context(SBT("S_b", [128, D], F32))]
    SS.Sbf = [es_ssd.enter_context(SBT("Sbf_f", [128, D], BF16)), es_ssd.enter_context(SBT("Sbf_b", [128, D], BF16))]
    SS.BS = [Buf("S_f"), Buf("S_b")]
    SS.BSbf = [Buf("Sbf_f"), Buf("Sbf_b")]
    SS.B_HB = Buf("HBdram")
    for d_ in range(2):
        K.op(dve, lambda e, d_=d_: e.memset(SS.S[d_][:], 0.0), [], [SS.BS[d_]])
        K.op(dve, lambda e, d_=d_: e.memset(SS.Sbf[d_][:], 0.0), [], [SS.BSbf[d_]])
    proj_phase([
        dict(rows=rows_of(x_in, 0, TMAIN + 1), T=TMAIN + 1, gm=gm1, sh=sh1, out_lo=0, out_hi=TMAIN,
             xbc_dst=lambda j: XBC[j, :, :], dt_dst=lambda t0, n: DTm[t0:t0 + n, :]),
    ], full=True)
    if stop_after <= 2:
        return finish(nc, K, es, out, sp)

    with ExitStack() as ph:
        def T_(name, shape, dt):
            return ph.enter_context(SBT(name, shape, dt))
        wbf = mk_wb(ph, "f")
        S, Sbf, BS, BSbf, B_HB = SS.S, SS.Sbf, SS.BS, SS.BSbf, SS.B_HB
        xin, dtt, Bxin, Bdtt = wbf.xin, wbf.dtt, wbf.Bxin, wbf.Bdtt
        dtA_, dth_, lnd_, sm_, Bsm_, xsT_, BxsT_ = wbf.dtA_, wbf.dth_, wbf.lnd_, wbf.sm_, wbf.Bsm_, wbf.xsT_, wbf.BxsT_
        if stop_after <= 3:
            K.barrier()
            ph.close()
            es_ssd.close()
            return finish(nc, K, es, out, sp)
        hb = [T_(f"hb{i}", [128, D], BF16) for i in range(2)]
        zs = [T_(f"zs{i}", [128, D], BF16) for i in range(2)]
        Bhb = [Buf("hb0"), Buf("hb1")]
        Bzs = [Buf("zs0"), Buf("zs1")]
        CBm = [[T_(f"CB{d_}{i}", [128, 1024], BF16) for d_ in range(2)] for i in range(2)]
        BCB = [[Buf("CB") for d_ in range(2)] for i in range(2)]
        xd = [[T_(f"xd{d_}{i}", [128, D], BF16) for d_ in range(2)] for i in range(2)]
        Bxd = [[Buf("xd") for d_ in range(2)] for i in range(2)]
        R = [T_("R_f", [128, 4096], BF16), T_("R_b", [128, 4096], BF16)]
        BR = [Buf("R_f"), Buf("R_b")]
        Eb = [T_(f"Eb{i}", [128, 512], BF16) for i in range(4)]
        BEb = [Buf(f"Eb{i}") for i in range(4)]
        Mt = [[T_(f"Mt{d_}{i}", [128, 4096], BF16) for d_ in range(2)] for i in range(2)]
        BMt = [[bl("Mt", 8) for d_ in range(2)] for i in range(2)]
        yt = T_("yt", [128, D], F32)
        t1 = T_("t1", [128, 512], F32)
        t2 = T_("t2", [128, 512], F32)
        Byt, Bt1, Bt2 = Buf("yt"), Buf("t1"), Buf("t2")
        ssq = T_("ssq", [128, 16], F32)
        Bssq = Buf("ssq")
        ysb = T_("ysb", [128, D], BF16)
        Bysb = Buf("ysb")
        yso = [T_(f"yso{i}", [128, 16, 128], BF16) for i in range(2)]
        Byso = [Buf("yso0"), Buf("yso1")]

        def load_main(c, slot):
            load_chunk(wbf, slot, xmsrc(c), 32, DTm[c * 128:(c + 1) * 128, :])
            K.dma(sp, hb[slot][:], HB[c], f"hbl{slot}", reads=[B_HB], writes=[Bhb[slot]])
            K.dma(sp, zs[slot][:], ZS[c * 128:(c + 1) * 128, :], f"zs{slot}", writes=[Bzs[slot]])
        masks_f = [cst[:, 1:2, :], cst[:, 2:3, :]]
        masks_b = [cstb[:, 1:2, :], cstb[:, 2:3, :]]
        segm = [cstb[:, 3, :], cstb[:, 4, :]]

        def front(c):
            slot = c % 2
            yield from prep_small(wbf, slot, full=True, bank=7)
            yield from prep_tm(wbf, slot, banks=(0, 1))
            dtA, sm, Bsm, xsT, BxsT, dth, lnd = (dtA_[slot], sm_[slot], Bsm_[slot], xsT_[slot], BxsT_[slot], dth_[slot],
                                                 lnd_[slot])
            for hb_ in range(2):
                bank = 2 + hb_

                def f(e, hb_=hb_, bank=bank):
                    for g4 in range(4):
                        g = hb_ * 4 + g4
                        i_ = e.matmul(PS[bank][:, g4 * 128:(g4 + 1) * 128], xin[slot][:, 16 + g, :], xin[slot][:, 24 + g, :],
                                      start=True, stop=True)
                    return i_
                K.op(pe, f, [Bxin[slot]], [PSB[bank]])
                yield
                for d_ in range(2):
                    K.op(dve, lambda e, d_=d_, hb_=hb_, bank=bank: e.tensor_tensor(
                        v3(CBm[slot][d_][:, hb_ * 512:(hb_ + 1) * 512], 4), v3(PS[bank][:, :], 4),
                        masks_f[d_].to_broadcast([128, 4, 128]), ALU.mult), [PSB[bank], B_cst], [BCB[slot][d_]])
                    yield
            for d_ in range(2):
                K.op(pool, lambda e, d_=d_: e.tensor_tensor(
                    v3(xd[slot][d_][:, :], 32), v3(xsT[:, :], 32),
                    dtt[slot][:, d_ * 32:(d_ + 1) * 32].unsqueeze(2).to_broadcast([128, 32, 64]), ALU.mult),
                    [BxsT, Bdtt[slot]], [Bxd[slot][d_]])
                yield
            for d_ in range(2):
                K.op(pool, lambda e, d_=d_: e.tensor_tensor(
                    v3(R[d_][:, :], 32), masks_b[d_].to_broadcast([128, 32, 128]),
                    dth[:, d_ * 32:(d_ + 1) * 32].unsqueeze(2).to_broadcast([128, 32, 128]), ALU.mult),
                    [Bsm, B_cst, BR[d_]], [BR[d_]])
                yield
            cnt = 0
            pend = []

            def emit_m(d_, g, eb):
                K.op(dve, lambda e: e.tensor_tensor(
                    v3(Mt[slot][d_][:, g * 512:(g + 1) * 512], 4), v3(Eb[eb][:, :], 4),
                    CBm[slot][d_][:, g * 128:(g + 1) * 128].unsqueeze(1).to_broadcast([128, 4, 128]), ALU.mult),
                    [BEb[eb], BCB[slot][d_]], [BMt[slot][d_][g]])
            for d_ in range(2):
                for g in range(8):
                    bank = 2 + cnt % 2
                    eb = cnt % 4
                    cnt += 1

                    def fs(e, d_=d_, g=g, bank=bank):
                        return e.matmul(PS[bank][:, :], segm[d_], R[d_][:, g * 512:(g + 1) * 512], start=True, stop=True)
                    K.op(pe, fs, [BR[d_], B_cst], [PSB[bank]])
                    yield
                    K.op(act, lambda e, bank=bank, eb=eb: e.activation(out=Eb[eb][:, :], in_=PS[bank][:, :], func=AF.Exp),
                         [PSB[bank]], [BEb[eb]])
                    yield
                    pend.append((d_, g, eb))
                    if len(pend) > 2:
                        emit_m(*pend.pop(0))
                        yield
            while pend:
                emit_m(*pend.pop(0))
                yield

        def back(c):
            slot = c % 2
            sm, Bsm, xsT, BxsT = sm_[slot], Bsm_[slot], xsT_[slot], BxsT_[slot]
            Mts, BMts, xds, Bxds = Mt[slot], BMt[slot], xd[slot], Bxd[slot]
            for gp in range(4):
                Y0, Y1, Y2 = 4, 5, 6

                def fy(e, gp=gp):
                    for g in (2 * gp, 2 * gp + 1):
                        for r in range(4):
                            hd = g * 4 + r
                            o = PS[Y0][:, (g % 2) * 256 + r * 64:(g % 2) * 256 + (r + 1) * 64]
                            e.matmul(o, Mts[0][:, g * 512 + r * 128:g * 512 + (r + 1) * 128], xds[0][:, hd * 64:(hd + 1) * 64],
                                     start=True, stop=False)
                            e.matmul(o, Mts[1][:, g * 512 + r * 128:g * 512 + (r + 1) * 128], xds[1][:, hd * 64:(hd + 1) * 64],
                                     start=False, stop=False)
                            i_ = e.matmul(o, DI[:, hd, :], xsT[:, hd * 64:(hd + 1) * 64], start=False, stop=True)
                    return i_
                K.op(pe, fy, [BMts[0][2 * gp], BMts[0][2 * gp + 1], BMts[1][2 * gp], BMts[1][2 * gp + 1], Bxds[0], Bxds[1],
                              BxsT, B_ssdc], [PSB[Y0]])
                yield

                def fo(e, gp=gp, which=0):
                    src = Sbf[0] if which == 0 else hb[slot]
                    bank = Y1 if which == 0 else Y2
                    for g in (2 * gp, 2 * gp + 1):
                        i_ = e.matmul(PS[bank][:, (g % 2) * 256:(g % 2 + 1) * 256], xin[slot][:, 24 + g, :],
                                      src[:, g * 256:(g + 1) * 256], start=True, stop=True)
                    return i_
                K.op(pe, lambda e, gp=gp: fo(e, gp, 0), [Bxin[slot], BSbf[0]], [PSB[Y1]])
                yield
                K.op(pe, lambda e, gp=gp: fo(e, gp, 1), [Bxin[slot], Bhb[slot]], [PSB[Y2]])
                yield
                ef = sm[:, 2, gp * 8:(gp + 1) * 8].unsqueeze(2).to_broadcast([128, 8, 64])
                ebk = sm[:, 2, 32 + gp * 8:32 + (gp + 1) * 8].unsqueeze(2).to_broadcast([128, 8, 64])
                K.op(dve, lambda e, ef=ef: e.tensor_tensor(v3(t1[:, :], 8), v3(PS[Y1][:, :], 8), ef, ALU.mult),
                     [PSB[Y1], Bsm], [Bt1])
                yield
                K.op(dve, lambda e, ebk=ebk: e.tensor_tensor(v3(t2[:, :], 8), v3(PS[Y2][:, :], 8), ebk, ALU.mult),
                     [PSB[Y2], Bsm], [Bt2])
                yield
                K.op(dve, lambda e: e.tensor_tensor(t1[:, :], t1[:, :], t2[:, :], ALU.add), [Bt1, Bt2], [Bt1])
                yield
                K.op(dve, lambda e, gp=gp: e.tensor_tensor(yt[:, gp * 512:(gp + 1) * 512], PS[Y0][:, :], t1[:, :], ALU.add),
                     [PSB[Y0], Bt1], [Byt])
                yield
            K.op(dve, lambda e: e.tensor_tensor(yt[:, :], yt[:, :], zs[slot][:, :], ALU.mult), [Byt, Bzs[slot]], [Byt])
            yield
            K.op(dve, lambda e: e.memset(ssq[:, :], 0.0), [], [Bssq])
            yield
            for g in range(8):
                K.op(act, lambda e, g=g: e.activation(out=ysb[:, g * 256:(g + 1) * 256], in_=yt[:, g * 256:(g + 1) * 256],
                                                      func=AF.Square, accum_out=ssq[:, g:g + 1]), [Byt], [Bysb, Bssq])
                yield
            K.op(dve, lambda e: e.tensor_scalar(ssq[:, 8:16], ssq[:, 0:8], 1.0 / 256, EPS, ALU.mult, ALU.add), [Bssq], [Bssq])
            yield
            K.op(act, lambda e: e.activation(out=ssq[:, 8:16], in_=ssq[:, 8:16], func=AF.Sqrt), [Bssq], [Bssq])
            yield
            K.op(dve, lambda e: e.reciprocal(ssq[:, 8:16], ssq[:, 8:16]), [Bssq], [Bssq])
            yield
            for g in range(8):
                K.op(act, lambda e, g=g: e.activation(out=ysb[:, g * 256:(g + 1) * 256], in_=yt[:, g * 256:(g + 1) * 256],
                                                      func=AF.Copy, scale=ssq[:, 8 + g:9 + g]), [Byt, Bssq], [Bysb])
                yield
            yo = c % 2
            for q in range(4):
                bank = 5 + q % 2

                def ft(e, q=q, bank=bank):
                    for b4 in range(4):
                        blk = q * 4 + b4
                        i_ = e.matmul(PS[bank][:, b4 * 128:(b4 + 1) * 128], ysb[:, blk * 128:(blk + 1) * 128], IDb,
                                      start=True, stop=True)
                    return i_
                K.op(pe, ft, [Bysb, B_cst], [PSB[bank]])
                yield
                K.op(dve, lambda e, q=q, bank=bank, yo=yo: e.tensor_tensor(
                    yso[yo][:, q * 4:(q + 1) * 4, :], v3(PS[bank][:, :], 4),
                    small["snw"][:, q * 4:(q + 1) * 4].unsqueeze(2).to_broadcast([128, 4, 128]), ALU.mult),
                    [PSB[bank], B_small], [Byso[yo]])
                yield
            K.dma(sp, YS[:, :, c * 128:(c + 1) * 128].rearrange("j p t -> p j t"), yso[yo][:], f"yso{yo}", reads=[Byso[yo]])
            if c + 1 < NCH_MAIN:
                yield from state_step(wbf, 0, slot, banks=(4, 5))

        load_main(0, 0)
        for _ in front(0):
            pass
        for c in range(NCH_MAIN):
            nxt = None
            if c + 1 < NCH_MAIN:
                load_main(c + 1, (c + 1) % 2)
                nxt = front(c + 1)
            if nxt is None:
                run(back(c))
            else:
                weave(back(c), nxt, 1.15)
        K.barrier()
    es_ssd.close()
    if stop_after <= 4:
        return finish(nc, K, es, out, sp)

    tiles2 = [(t0, min(512, T2 - t0)) for t0 in range(0, T2, 512)]
    es_yin = ExitStack()
    yin = es_yin.enter_context(SBT("yin", [128, 16, T2], BF16))
    Byin = [Buf(f"yin{i}") for i in range(len(tiles2))]
    with ExitStack() as ph:
        def T_(name, shape, dt):
            return ph.enter_context(SBT(name, shape, dt))
        pwb = T_("pwb", [128, 4, 4, 512], BF16)
        Bpw = Buf("pwb")
        for g in range(4):
            K.dma(pool, pwb[:, g, :, :], poolw[g].rearrange("(kc p) o -> p kc o", p=128), "pwb", writes=[Bpw])
        vt = [T_(f"vt{i}", [128, 4, D], BF16) for i in range(2)]
        Bvt = [Buf("vt0"), Buf("vt1")]
        pooled_ = [T_(f"pooled{i}", [128, 16, 512], BF16) for i in range(2)]
        Bpl_ = [bl("pl", 16), bl("pl", 16)]
        ypo = [T_(f"ypo{i}", [128, 512], BF16) for i in range(2)]
        Bypo = [Buf("ypo0"), Buf("ypo1")]
        cnt = 0
        def vload(ti):
            t0, n = tiles2[ti]
            for q in range((n + 127) // 128):
                nn = min(128, n - q * 128)
                K.dma(sp, vt[ti % 2][0:nn, q, :], VV[t0 + q * 128:t0 + q * 128 + nn, :], f"vt{ti % 2}", writes=[Bvt[ti % 2]])
        vload(0)
        for ti, (t0, n) in enumerate(tiles2):
            s_ = ti % 2
            pooled, Bpl = pooled_[s_], Bpl_[s_]
            nq = (n + 127) // 128
            if ti + 1 < len(tiles2):
                vload(ti + 1)
            if ti == 0:
                for i_, (t0_, n_) in enumerate(tiles2):
                    K.dma(sp, yin[:, :, t0_:t0_ + n_], YS[:, :, t0_:t0_ + n_].rearrange("j p t -> p j t"), f"yin{i_ % 2}",
                          writes=[Byin[i_]])
            for blk in range(16):
                bank = 4 + cnt % 4
                cnt += 1

                def f(e, blk=blk, bank=bank, s_=s_, nq=nq, n=n):
                    for q in range(nq):
                        nn = min(128, n - q * 128)
                        i_ = e.matmul(PS[bank][:, q * 128:q * 128 + nn], vt[s_][0:nn, q, blk * 128:(blk + 1) * 128],
                                      cstb[0:nn, 5 + blk // 4, 0:nn], start=True, stop=True)
                    return i_
                K.op(pe, f, [Bvt[s_], B_cst], [PSB[bank]])
                K.op(act, lambda e, blk=blk, bank=bank, n=n, pooled=pooled: e.activation(
                    out=pooled[:, blk, 0:n], in_=PS[bank][:, 0:n], func=AF.Copy), [PSB[bank]], [Bpl[blk]])
            for j in range(16):
                g = j // 4
                bank = 4 + cnt % 4
                cnt += 1
                r = j % 2

                def f(e, j=j, g=g, bank=bank, n=n, pooled=pooled):
                    for kc in range(4):
                        i_ = e.matmul(PS[bank][:, 0:n], pwb[:, g, kc, (j % 4) * 128:(j % 4 + 1) * 128],
                                      pooled[:, g * 4 + kc, 0:n], start=(kc == 0), stop=(kc == 3))
                    return i_
                K.op(pe, f, [Bpw] + Bpl[g * 4:g * 4 + 4], [PSB[bank]])
                K.op(act, lambda e, j=j, bank=bank, n=n, r=r: e.activation(
                    out=ypo[r][:, 0:n], in_=PS[bank][:, 0:n], func=AF.Copy, scale=small["pscale"][:, j:j + 1]),
                    [PSB[bank], B_small], [Bypo[r]])
                K.dma(sp, YP[j, :, t0:t0 + n], ypo[r][:, 0:n], f"ypo{r}", reads=[Bypo[r]])
        K.barrier()
    if stop_after <= 5:
        return finish(nc, K, es, out, sp)

    with ExitStack() as ph:
        def T_(name, shape, dt):
            return ph.enter_context(SBT(name, shape, dt))
        merged = T_("merged", [128, 16, T2], BF16)
        w5 = [T_(f"w5{i}", [128, 16, 512], BF16) for i in range(2)]
        Bw5 = [Buf("w50"), Buf("w51")]
        Bmg = Buf("merged")
        gt = [T_(f"gt{i}", [128, 512], BF16) for i in range(2)]
        Bgt = [Buf("gt0"), Buf("gt1")]
        tmpf = T_("tmpf", [128, 512], F32)
        Btmp = Buf("tmpf")
        cg = [0]

        def mk(gsrc, first):
            def epi(j, ti, t0, n, bank):
                r = cg[0] % 2
                cg[0] += 1
                K.dma(sp, gt[r][:, 0:n], gsrc[j, :, t0:t0 + n], f"gt{r}", writes=[Bgt[r]])
                if first:
                    K.op(dve, lambda e: e.tensor_tensor(merged[:, j, t0:t0 + n], PS[bank][:, 0:n], gt[r][:, 0:n], ALU.mult),
                         [PSB[bank], Bgt[r]], [Bmg])
                else:
                    K.op(dve, lambda e: e.tensor_tensor(tmpf[:, 0:n], PS[bank][:, 0:n], gt[r][:, 0:n], ALU.mult),
                         [PSB[bank], Bgt[r]], [Btmp])
                    K.op(dve, lambda e: e.tensor_tensor(merged[:, j, t0:t0 + n], merged[:, j, t0:t0 + n], tmpf[:, 0:n],
                                                        ALU.add), [Btmp, Bmg], [Bmg])
            return epi
        with ExitStack() as sub:
            run(fm_gemm(sub, w_so, 0, 16, 16, lambda kc, t0, n: (yin[:, kc, t0:t0 + n], [Byin[t0 // 512]]), tiles2, mk(GS, True), "so",
                        wset=(w5, Bw5)))
        for i_, (t0_, n_) in enumerate(tiles2):
            K.dma(act, yin[:, :, t0_:t0_ + n_], YP[:, :, t0_:t0_ + n_].rearrange("j p t -> p j t"), f"yin{i_ % 2}",
                  writes=[Byin[i_]])
        with ExitStack() as sub:
            run(fm_gemm(sub, w_po, 0, 16, 16, lambda kc, t0, n: (yin[:, kc, t0:t0 + n], [Byin[t0 // 512]]), tiles2, mk(GP, False), "po",
                        wset=(w5, Bw5)))
        K.fence()
        g1t = T_("g1t", [128, D], F32)
        Bg1 = Buf("g1t")
        K.dma(sp, g1t[:], GBS[:, 0, :], "g1t", writes=[Bg1])
        xr = [T_(f"xr{i}", [128, 512], F32) for i in range(3)]
        xo2 = [T_(f"xo2{i}", [128, 512], F32) for i in range(2)]
        Bxr = [Buf("xr0"), Buf("xr1"), Buf("xr2")]
        Bxo2 = [Buf("xo20"), Buf("xo21")]
        chunks2 = [(t0, min(128, T2 - t0)) for t0 in range(0, T2, 128)]

        def pre_o(idx, ct, c):
            t0, n = c
            K.dma(sp, xr[idx % 3][0:n, :], x_in[t0:t0 + n, ct * 512:ct * 512 + 512], f"xr{idx % 3}", writes=[Bxr[idx % 3]])

        def epi_o(idx, ct, c, n, w, bank):
            t0 = c[0]
            r = idx % 2
            xr_, Bxr_ = xr[idx % 3], Bxr[idx % 3]
            K.op(dve, lambda e: e.tensor_tensor(xo2[r][0:n, 0:w], PS[bank][0:n, 0:w], g1t[0:n, ct * 512:ct * 512 + w],
                                                ALU.mult), [PSB[bank], Bg1], [Bxo2[r]])
            K.op(dve, lambda e: e.tensor_tensor(xo2[r][0:n, 0:w], xo2[r][0:n, 0:w], xr_[0:n, 0:w], ALU.add),
                 [Bxo2[r], Bxr_], [Bxo2[r]])
            K.dma(sp, X2[t0:t0 + n, ct * 512:ct * 512 + w], xo2[r][0:n, 0:w], f"xo2{r}", reads=[Bxo2[r]])
        with ExitStack() as sub:
            run(tm_gemm(sub, w_o, 0, D, 16, lambda kc, c: (merged[:, kc, c[0]:c[0] + c[1]], [Bmg], c[1]), chunks2, epi_o, "wo",
                    pre=pre_o, wset=(w5, Bw5)))
            K.barrier()
    es_yin.close()
    if stop_after <= 6:
        return finish(nc, K, es, out, sp)

    for half in range(2):
        o0 = half * 1024
        lo = max(0, o0 - 64)
        hi = o0 + 1024 + 64
        Th = hi - lo
        with ExitStack() as ph:
            def T_(name, shape, dt):
                return ph.enter_context(SBT(name, shape, dt))
            U = T_("U", [128, 44, 1024], BF16)
            BU = bl("U", 44)
            with ExitStack() as sub:
                def S_(name, shape, dt):
                    return sub.enter_context(SBT(name, shape, dt))
                H2 = S_("H2", [128, 16, Th], BF16)
                BH2 = Buf("H2")
                with ExitStack() as ph2:
                    norm_mod_T(ph2, rows_of(X2, lo, hi), gm2, sh2, H2, BH2, 0, f"f{half}")
                    K.barrier()
                graw = [S_(f"graw{i}", [128, 64 + Th], BF16) for i in range(2)]
                Bgr = [Buf("graw0"), Buf("graw1")]
                for i in range(2):
                    K.op(dve, lambda e, i=i: e.memset(graw[i][:], 0.0), [], [Bgr[i]])
                facc = S_("facc", [128, 1024], F32)
                fgl = S_("fgl", [128, 1024], BF16)
                Bfa, Bfg = Buf("facc"), Buf("fgl")
                wa_ = [S_(f"wua{i}", [128, 16, 256], BF16) for i in range(2)]
                wg_ = [S_(f"wug{i}", [128, 16, 256], BF16) for i in range(2)]
                Bwa = [Buf("wua0"), Buf("wua1")]
                Bwg = [Buf("wug0"), Buf("wug1")]
                gtiles = [(t0, min(512, Th - t0)) for t0 in range(0, Th, 512)]
                cnt = 0
                fw, fb = small["fcw"], small["fcb"]
                for g in range(22):
                    s_ = g % 2
                    K.dma(pool, wa_[s_][:], w_up[:, g * 256:(g + 1) * 256].rearrange("(kc p) n -> p kc n", p=128),
                          f"wua{s_}", writes=[Bwa[s_]])
                    K.dma(pool, wg_[s_][:], w_up[:, DFF + g * 256:DFF + (g + 1) * 256].rearrange("(kc p) n -> p kc n", p=128),
                          f"wug{s_}", writes=[Bwg[s_]])
                    for jb in range(2):
                        j = g * 2 + jb
                        r = j % 2
                        for (t0, n) in gtiles:
                            bank = 4 + cnt % 4
                            cnt += 1

                            def f(e, s_=s_, jb=jb, bank=bank, t0=t0, n=n):
                                for kc in range(16):
                                    i_ = e.matmul(PS[bank][:, 0:n], wg_[s_][:, kc, jb * 128:(jb + 1) * 128],
                                                  H2[:, kc, t0:t0 + n], start=(kc == 0), stop=(kc == 15))
                                return i_
                            K.op(pe, f, [Bwg[s_], BH2], [PSB[bank]])
                            K.op(act, lambda e, r=r, bank=bank, t0=t0, n=n: e.activation(
                                out=graw[r][:, 64 + t0:64 + t0 + n], in_=PS[bank][:, 0:n], func=AF.Copy),
                                [PSB[bank]], [Bgr[r]])
                        cc = 64 + o0 - lo
                        K.op(dve, lambda e, r=r, j=j, cc=cc: e.tensor_scalar(
                            facc[:, :], graw[r][:, cc:cc + 1024], fw[:, j, 1:2], fb[:, j:j + 1], ALU.mult, ALU.add),
                            [Bgr[r], B_small], [Bfa])
                        K.op(dve, lambda e, r=r, j=j, cc=cc: e.scalar_tensor_tensor(
                            facc[:, :], graw[r][:, cc - 64:cc - 64 + 1024], fw[:, j, 0:1], facc[:, :], ALU.mult, ALU.add),
                            [Bgr[r], B_small, Bfa], [Bfa])
                        K.op(dve, lambda e, r=r, j=j, cc=cc: e.scalar_tensor_tensor(
                            facc[:, :], graw[r][:, cc + 64:cc + 64 + 1024], fw[:, j, 2:3], facc[:, :], ALU.mult, ALU.add),
                            [Bgr[r], B_small, Bfa], [Bfa])
                        K.op(act, lambda e: e.activation(out=fgl[:, :], in_=facc[:, :], func=AF.Gelu), [Bfa], [Bfg])
                        for ta in range(2):
                            bank = 4 + cnt % 4
                            cnt += 1
                            a0 = o0 - lo + ta * 512

                            def f(e, s_=s_, jb=jb, bank=bank, a0=a0):
                                for kc in range(16):
                                    i_ = e.matmul(PS[bank][:, :], wa_[s_][:, kc, jb * 128:(jb + 1) * 128],
                                                  H2[:, kc, a0:a0 + 512], start=(kc == 0), stop=(kc == 15))
                                return i_
                            K.op(pe, f, [Bwa[s_], BH2], [PSB[bank]])
                            K.op(dve, lambda e, j=j, ta=ta, bank=bank: e.tensor_tensor(
                                U[:, j, ta * 512:(ta + 1) * 512], PS[bank][:, :], fgl[:, ta * 512:(ta + 1) * 512], ALU.mult),
                                [PSB[bank], Bfg], [BU[j]])
                K.barrier()
            g2t = T_("g2t", [128, D], F32)
            K.dma(sp, g2t[:], GBS[:, 1, :], "g2t", writes=[B_gb])
            xr = [T_(f"xr{i}", [128, 256], F32) for i in range(3)]
            xo3 = [T_(f"xo3{i}", [128, 256], F32) for i in range(2)]
            Bxr = [Buf("xr0"), Buf("xr1"), Buf("xr2")]
            Bxo3 = [Buf("xo30"), Buf("xo31")]
            chunksd = [(t0, 128) for t0 in range(0, 1024, 128)]

            def pre_d(idx, ct, c, o0=o0):
                t0 = o0 + c[0]
                K.dma(sp, xr[idx % 3][:, :], X2[t0:t0 + 128, ct * 256:ct * 256 + 256], f"xr{idx % 3}", writes=[Bxr[idx % 3]])

            def epi_d(idx, ct, c, n, w, bank, o0=o0):
                t0 = o0 + c[0]
                r = idx % 2
                xr_, Bxr_ = xr[idx % 3], Bxr[idx % 3]
                K.op(dve, lambda e: e.tensor_tensor(xo3[r][:, 0:w], PS[bank][:, 0:w], g2t[:, ct * 256:ct * 256 + w], ALU.mult),
                     [PSB[bank], B_gb], [Bxo3[r]])
                K.op(dve, lambda e: e.tensor_tensor(xo3[r][:, 0:w], xo3[r][:, 0:w], xr_[:, 0:w], ALU.add),
                     [Bxo3[r], Bxr_], [Bxo3[r]])
                K.dma(sp, X3[t0:t0 + n, ct * 256:ct * 256 + w], xo3[r][:, 0:w], f"xo3{r}", reads=[Bxo3[r]])
            with ExitStack() as sub:
                run(tm_gemm(sub, w_dn, 0, D, 44, lambda kc, c: (U[:, kc, c[0]:c[0] + 128], [BU[kc]], 128), chunksd, epi_d,
                        f"wd{half}", nbuf=2, cw=256, pre=pre_d))
                K.barrier()
    if stop_after <= 7:
        return finish(nc, K, es, out, sp)

    with ExitStack() as ph:
        def T_(name, shape, dt):
            return ph.enter_context(SBT(name, shape, dt))
        fw_ = T_("fnwt", [128, D], F32)
        Bfw = Buf("fnw")
        K.dma(sp, fw_[:], fnw, "fnw", writes=[Bfw])
        NS = 3
        xt = [T_(f"fx{i}", [128, D], F32) for i in range(NS)]
        yo_ = [T_(f"fy{i}", [128, D], F32) for i in range(2)]
        st = [T_(f"fs{i}", [128, 2], F32) for i in range(NS)]
        junk = T_("fj", [128, D], BF16)
        Bx = [Buf(f"fx{i}") for i in range(NS)]
        By = [Buf("fy0"), Buf("fy1")]
        Bs = [Buf(f"fs{i}") for i in range(NS)]
        Bj = Buf("fj")

        def fload(i):
            K.dma(sp, xt[i % NS][:], X3[i * 128:(i + 1) * 128, :], f"fx{i % NS}", writes=[Bx[i % NS]])

        def fA(i):
            s_ = i % NS
            K.op(dve, lambda e: e.memset(st[s_][:, :], 0.0), [], [Bs[s_]])
            yield
            K.op(act, lambda e: e.activation(out=junk[:, :], in_=xt[s_][:, :], func=AF.Square, accum_out=st[s_][:, 0:1]),
                 [Bx[s_]], [Bj, Bs[s_]])
            yield
            K.op(dve, lambda e: e.tensor_scalar(st[s_][:, 1:2], st[s_][:, 0:1], 1.0 / D, EPS, ALU.mult, ALU.add),
                 [Bs[s_]], [Bs[s_]])
            yield
            K.op(act, lambda e: e.activation(out=st[s_][:, 1:2], in_=st[s_][:, 1:2], func=AF.Sqrt), [Bs[s_]], [Bs[s_]])
            yield
            K.op(dve, lambda e: e.reciprocal(st[s_][:, 1:2], st[s_][:, 1:2]), [Bs[s_]], [Bs[s_]])
            yield

        def fB(i):
            s_ = i % NS
            y_ = i % 2
            K.op(act, lambda e: e.activation(out=yo_[y_][:, :], in_=xt[s_][:, :], func=AF.Copy, scale=st[s_][:, 1:2]),
                 [Bx[s_], Bs[s_]], [By[y_]])
            yield
            K.op(dve, lambda e: e.tensor_tensor(yo_[y_][:, :], yo_[y_][:, :], fw_[:, :], ALU.mult), [By[y_], Bfw], [By[y_]])
            yield
            K.dma(sp, out[i * 128:(i + 1) * 128, :], yo_[y_][:, :], f"fy{y_}", reads=[By[y_]])
            yield
        fload(0)
        fload(1)
        for _ in fA(0):
            pass
        for i in range(16):
            if i + 2 < 16:
                fload(i + 2)
            _interleave(fB(i), fA(i + 1) if i + 1 < 16 else None)
        K.barrier()
    return finish(nc, K, es, out, sp)


def finish(nc, K, es, out, sp):
    K.barrier()
    es.close()
    return nc


POOL_WINDOWS = (2, 4, 8, 16)


def _consts(mirror):
    c = np.zeros((128, 10, 128), np.float32)
    k = np.arange(128)[:, None]
    l = np.arange(128)[None, :]
    c[:, 0, :] = (k == l)
    c[:, 1, :] = (k <= l)
    c[:, 2, :] = (k >= l)
    c[:, 3, :] = (k > l)
    c[:, 4, :] = (k < l)
    for wi, w in enumerate(POOL_WINDOWS):
        P = np.zeros((64, 64), np.float32)
        for t in range(64):
            s0 = max(t - w // 2, 0)
            e0 = min(t + w - w // 2, 64)
            P[s0:e0, t] = 1.0 / (e0 - s0)
            P[t, t] -= 1.0
        if mirror:
            P = P[::-1, ::-1]
        c[0:64, 5 + wi, 0:64] = P
        c[64:128, 5 + wi, 64:128] = P
    c[:, 9, :] = 1.0
    return c


def _pp(v, nblk):
    return np.ascontiguousarray(np.asarray(v, np.float32).reshape(nblk, 128).T)


def _rep(v):
    return np.ascontiguousarray(np.broadcast_to(np.asarray(v, np.float32).reshape(1, -1), (128, v.size)))


def make_in_maps(x, c, ctx, c_ctx, w_ada, b_ada, norm1_w, w_in, ssd_conv_w, ssd_conv_b, dt_bias, a_log,
                 d_skip, ssd_norm_w, w_ssd_out, pool_w, pool_scale, w_pool_out, w_o, norm2_w, w_up,
                 ffn_conv_w, ffn_conv_b, w_down, final_norm_w, cores=range(8)):
    f = lambda a: np.ascontiguousarray(np.asarray(a, np.float32))
    w_in0 = f(w_in[0])
    w_in1 = w_in0.copy()
    w_in1[:, 0:32] = w_in0[:, 32:64]
    w_in1[:, 32:64] = w_in0[:, 0:32]
    shared = dict(
        w_ada=f(w_ada[0]), b_ada_pp=_pp(b_ada[0], 96),
        b_ada_g=np.ascontiguousarray(np.stack([_rep(b_ada[0][2 * D:3 * D]), _rep(b_ada[0][5 * D:6 * D])], axis=1)),
        n1w=_pp(norm1_w[0], 16), n2w=_pp(norm2_w[0], 16), convb=_pp(ssd_conv_b[0], 32),
        snw=_pp(ssd_norm_w[0], 16), w_so=f(w_ssd_out[0]), poolw=f(pool_w[0]), pscale=_pp(pool_scale[0], 16),
        w_po=f(w_pool_out[0]), w_o=f(w_o[0]), w_up=f(w_up[0]), fcb=_pp(ffn_conv_b[0], 44), w_dn=f(w_down[0]),
        fnw=_rep(final_norm_w),
    )
    maps = []
    for core in cores:
        b, m = core // 2, core % 2
        d = dict(shared)
        xs = f(x[b])
        cx = f(ctx[b])
        cw = np.asarray(ssd_conv_w[0], np.float32)
        fw = np.asarray(ffn_conv_w[0], np.float32)
        dirs = [0, 1]
        if m:
            xs = np.ascontiguousarray(xs[::-1])
            cx = np.ascontiguousarray(cx[::-1])
            cw = cw[::-1]
            fw = fw[::-1]
            dirs = [1, 0]
        d["x"] = xs
        d["ctx"] = cx
        d["w_in"] = w_in1 if m else w_in0
        d["cvec"] = np.ascontiguousarray(np.stack([_pp(c[b], 16), _pp(c_ctx, 16)], axis=2))
        d["convw"] = np.ascontiguousarray(np.stack([_pp(cw[t], 32) for t in range(3)], axis=2))
        d["fcw"] = np.ascontiguousarray(np.stack([_pp(fw[t], 44) for t in range(3)], axis=2))
        cat = lambda a: _rep(np.concatenate([np.asarray(a[0][dirs[0]]), np.asarray(a[0][dirs[1]])]))
        d["dtb"] = cat(dt_bias)
        d["alog"] = cat(a_log)
        d["dsk"] = cat(d_skip)
        d["consts"] = _consts(bool(m))
        maps.append(d)
    return maps


_NC_CACHE = {}


def kernel(**inputs):
    if "nc" not in _NC_CACHE:
        _NC_CACHE["nc"] = build()
    nc = _NC_CACHE["nc"]
    maps = make_in_maps(**inputs)
    res = run_bass_kernel_spmd(nc, maps, core_ids=list(range(8)))
    B = inputs["x"].shape[0]
    out = np.zeros((B, L, D), np.float32)
    for core in range(8):
        b, m = core // 2, core % 2
        o = np.asarray(res.results[core]["out"], np.float32)
        if m:
            out[b, TOWN:] = o[::-1]
        else:
            out[b, :TOWN] = o
    return out
```
